# Optimizing a Trainium2 kernel written in Bass

```python
import jax, jax.numpy as jnp
from jax import lax
import numpy as np

D_MODEL = 2048
BATCH = 2
SEQ = 16384
DEPTH = 4

N_MIXERS = 2
HEAD_DIM = 128
ATTN_SLOTS = 16
ATTN_WIDTH = ATTN_SLOTS * HEAD_DIM
DILATION_GROUPS = ((128, 1), (512, 4), (2048, 16))
N_GROUPS = len(DILATION_GROUPS)
QKV_WIDTH = 3 * N_GROUPS * ATTN_WIDTH
ATTN_IN_WIDTH = QKV_WIDTH + ATTN_WIDTH
BLOCK_Q = 128
CONV_WIDTH = D_MODEL
CONV_K = 3
CONV_IN_WIDTH = 4 * CONV_WIDTH
NORM_EPS = 1e-6

kernel_name = "hybrid_dilated_swa_shortconv"


def alibi_slopes():
    n = N_GROUPS * ATTN_SLOTS
    s = 2.0 ** (-8.0 * np.arange(1, n + 1) / n)
    return s.reshape(N_GROUPS, ATTN_SLOTS).astype(np.float32)


def rmsnorm(x, g):
    xf = x.astype(jnp.float32)
    y = xf * lax.rsqrt(jnp.mean(xf * xf, axis=-1, keepdims=True) + NORM_EPS) * g.astype(jnp.float32)
    return y.astype(x.dtype)


def to_residues(t, d):
    b, s = t.shape[:2]
    rest = t.shape[2:]
    t = t.reshape((b, s // d, d) + rest)
    t = jnp.moveaxis(t, 2, 1)
    return t.reshape((b * d, s // d) + rest)


def from_residues(t, b, d):
    L = t.shape[1]
    rest = t.shape[2:]
    t = t.reshape((b, d, L) + rest)
    t = jnp.moveaxis(t, 1, 2)
    return t.reshape((b, L * d) + rest)


def band_attention(q, k, v, slopes, window, dilation):
    n, L, h, dh = q.shape
    nblk = -(-L // BLOCK_Q)
    lp = nblk * BLOCK_Q
    pad_end = lp - L
    qb = jnp.pad(q, ((0, 0), (0, pad_end), (0, 0), (0, 0))).reshape(n, nblk, BLOCK_Q, h, dh)

    def band(t):
        tp = jnp.pad(t, ((0, 0), (BLOCK_Q, pad_end), (0, 0), (0, 0))).reshape(n, nblk + 1, BLOCK_Q, h, dh)
        return jnp.concatenate([tp[:, :-1], tp[:, 1:]], axis=2)

    kb, vb = band(k), band(v)
    s = jnp.einsum('ncqhd,nckhd->nchqk', qb, kb, preferred_element_type=jnp.float32) * (dh ** -0.5)
    qi = jnp.arange(BLOCK_Q)[:, None]
    kj = jnp.arange(2 * BLOCK_Q)[None, :]
    dist = qi + BLOCK_Q - kj
    in_window = (dist >= 0) & (dist <= window)
    key_idx = jnp.arange(nblk)[:, None, None] * BLOCK_Q - BLOCK_Q + kj[None]
    valid = in_window[None] & (key_idx >= 0)
    bias = -(slopes[:, None, None] * (dilation * dist).astype(jnp.float32)[None])
    s = jnp.where(valid[None, :, None], s + bias[None, None], -jnp.inf)
    lse = jax.nn.logsumexp(s, axis=-1)
    p = jnp.exp(s - lse[..., None])
    o = jnp.einsum('nchqk,nckhd->ncqhd', p.astype(v.dtype), vb, preferred_element_type=jnp.float32)
    o = o.reshape(n, lp, h, dh)[:, :L]
    lse = jnp.swapaxes(lse, 2, 3).reshape(n, lp, h)[:, :L]
    return o, lse


def dilated_attention_mixer(x, w_in, w_out):
    b, s, _ = x.shape
    proj = x @ w_in
    qkv = proj[..., :QKV_WIDTH].reshape(b, s, 3, N_GROUPS, ATTN_SLOTS, HEAD_DIM)
    gate = proj[..., QKV_WIDTH:]
    slopes = jnp.asarray(alibi_slopes())
    outs, lses = [], []
    for g, (window, dil) in enumerate(DILATION_GROUPS):
        q = to_residues(qkv[:, :, 0, g], dil)
        k = to_residues(qkv[:, :, 1, g], dil)
        v = to_residues(qkv[:, :, 2, g], dil)
        o, lse = band_attention(q, k, v, slopes[g], window // dil, dil)
        outs.append(from_residues(o, b, dil))
        lses.append(from_residues(lse, b, dil))
    wts = jax.nn.softmax(jnp.stack(lses), axis=0)
    o = jnp.sum(wts[..., None] * jnp.stack(outs), axis=0)
    y = o.reshape(b, s, ATTN_WIDTH).astype(x.dtype) * jax.nn.silu(gate)
    return y @ w_out


def short_conv_mixer(x, w_in, conv_w, w_out):
    s = x.shape[1]
    proj = x @ w_in
    h, bg, cg, gate = jnp.split(proj, 4, axis=-1)
    u = cg * h
    up = jnp.pad(u, ((0, 0), (CONV_K - 1, 0), (0, 0)))
    conv = conv_w[0] * up[:, 0:s] + conv_w[1] * up[:, 1:s + 1] + conv_w[2] * up[:, 2:s + 2]
    y = bg * conv * jax.nn.silu(gate)
    return y @ w_out


def setup_inputs(seed: int = 0) -> dict:
    key = jax.random.key(seed)
    keys = jax.random.split(key, 1 + 8 * DEPTH)
    out = {"x": jax.random.normal(keys[0], (BATCH, SEQ, D_MODEL), jnp.float32)}
    ki = 1
    for i in range(DEPTH):
        kind = i % N_MIXERS
        out[f"l{i}_norm_pre"] = 1.0 + 0.1 * jax.random.normal(keys[ki], (D_MODEL,), jnp.float32); ki += 1
        if kind == 0:
            out[f"l{i}_w_in"] = jax.random.normal(keys[ki], (D_MODEL, ATTN_IN_WIDTH), jnp.float32) * D_MODEL ** -0.5; ki += 1
            out[f"l{i}_w_out"] = jax.random.normal(keys[ki], (ATTN_WIDTH, D_MODEL), jnp.float32) * ATTN_WIDTH ** -0.5; ki += 1
        else:
            out[f"l{i}_w_in"] = jax.random.normal(keys[ki], (D_MODEL, CONV_IN_WIDTH), jnp.float32) * D_MODEL ** -0.5; ki += 1
            out[f"l{i}_conv_w"] = jax.random.normal(keys[ki], (CONV_K, CONV_WIDTH), jnp.float32) * CONV_K ** -0.5; ki += 1
            out[f"l{i}_w_out"] = jax.random.normal(keys[ki], (CONV_WIDTH, D_MODEL), jnp.float32) * CONV_WIDTH ** -0.5; ki += 1
        out[f"l{i}_norm_post"] = 1.0 + 0.1 * jax.random.normal(keys[ki], (D_MODEL,), jnp.float32); ki += 1
    return out


def reference(x,
              l0_norm_pre, l0_w_in, l0_w_out, l0_norm_post,
              l1_norm_pre, l1_w_in, l1_conv_w, l1_w_out, l1_norm_post,
              l2_norm_pre, l2_w_in, l2_w_out, l2_norm_post,
              l3_norm_pre, l3_w_in, l3_conv_w, l3_w_out, l3_norm_post):
    layers = [
        (l0_norm_pre, (l0_w_in, l0_w_out), l0_norm_post),
        (l1_norm_pre, (l1_w_in, l1_conv_w, l1_w_out), l1_norm_post),
        (l2_norm_pre, (l2_w_in, l2_w_out), l2_norm_post),
        (l3_norm_pre, (l3_w_in, l3_conv_w, l3_w_out), l3_norm_post),
    ]
    for i in range(DEPTH):
        g_pre, params, g_post = layers[i]
        hN = rmsnorm(x, g_pre)
        if i % N_MIXERS == 0:
            y = dilated_attention_mixer(hN, *params)
        else:
            y = short_conv_mixer(hN, *params)
        x = x + rmsnorm(y, g_post)
    return x
```

```python
import contextlib
import numpy as np
import ml_dtypes
import concourse.bass as bass
import concourse.mybir as mybir
from concourse.bass_utils import run_bass_kernel_spmd

F32 = mybir.dt.float32
BF16 = mybir.dt.bfloat16
AF = mybir.ActivationFunctionType
ALU = mybir.AluOpType

D = 2048
KC = D // 128
NCORES = 8
SEQ = 16384
BATCH = 2
T_OWN = 4096
HALO = 2048
EPS = 1e-6
HEADS = 16
DH = 128
GROUPS = ((128, 1), (512, 4), (2048, 16))
NG = 3
QKVW = 3 * NG * D


def alibi_slopes():
    n = NG * HEADS
    s = 2.0 ** (-8.0 * np.arange(1, n + 1) / n)
    return s.reshape(NG, HEADS).astype(np.float32)


def bias_table():
    sl = alibi_slopes()
    k = np.arange(128)[:, None].astype(np.float64)
    q = np.arange(128)[None, :].astype(np.float64)
    out = np.zeros((NG, HEADS, 128, 256), np.float32)
    for g, (_, dil) in enumerate(GROUPS):
        for h in range(HEADS):
            dist_cur = q - k
            dist_prev = q + 128 - k
            b_cur = np.where(dist_cur >= 0, -sl[g, h] * dil * dist_cur, -200.0)
            b_prev = np.where(dist_prev <= 128, -sl[g, h] * dil * dist_prev, -200.0)
            out[g, h, :, 0:128] = b_cur
            out[g, h, :, 128:256] = b_prev
    return out.reshape(NG * HEADS, 128, 256)


class Cnt:
    def __init__(self, sem):
        self.sem = sem
        self.n = 0


def _sems(es, nc, names):
    return {n: es.enter_context(nc.semaphore(n)) for n in names}


class CS:
    def __init__(self, sem, per=1, cnt=1):
        self.sem, self.per, self.cnt, self.n = sem, per, cnt, 0

    def tick(self, cnt=None):
        self.n += self.per * (self.cnt if cnt is None else cnt)
        return self.n


class Prog:
    ENG = ("sync", "scalar", "vector", "gpsimd", "tensor")

    def __init__(self, nc, es, pfx):
        self.nc, self.es, self.pfx = nc, es, pfx
        self.q = {k: [] for k in self.ENG}
        self.nsem = 0

    def cs(self, per=1, cnt=1):
        self.nsem += 1
        sem = self.es.enter_context(self.nc.semaphore(f"{self.pfx}s{self.nsem}"))
        return CS(sem, per, cnt)

    def dcs(self, cnt=1):
        return self.cs(16, cnt)

    def op(self, eng, fn, waits=(), sig=None, cnt=None):
        waits = [(c, t) for (c, t) in waits if c is not None and t is not None and t > 0]
        if sig is not None and cnt is None:
            cnt = sig.cnt
        tk = sig.tick(cnt) if sig is not None else None

        def emit(e, fn=fn, waits=waits, sig=sig, cnt=cnt):
            for c, t in waits:
                e.wait_ge(c.sem, t)
            ins = fn(e) if fn is not None else None
            if sig is not None:
                if not isinstance(ins, (list, tuple)):
                    ins = [ins]
                assert len(ins) == cnt, (len(ins), cnt)
                for i_ in ins:
                    i_.then_inc(sig.sem, sig.per)

        self.q[eng].append(emit)
        return tk

    def run(self):
        blk = self.es.enter_context(self.nc.Block())
        for name in self.ENG:
            fns = self.q[name]
            getattr(blk, name)(lambda e, fns=fns: [f(e) for f in fns])


def phase_c(nc, pfx, yT_d, x_ap, wout_d, gpost_d, out_ap, T):
    NT = T // 128
    NGp = T // 512
    with contextlib.ExitStack() as es:
        sb = lambda n, s, d: es.enter_context(nc.sbuf_tensor(pfx + n, s, d))
        wo = sb("wo", [128, KC, D], BF16)
        gp = sb("gp", [128, D], F32)
        yt = [sb(f"yt{i}", [128, KC, 512], BF16) for i in range(2)]
        xt = [sb(f"xt{i}", [128, D], F32) for i in range(3)]
        tt = [sb(f"tt{i}", [128, D], F32) for i in range(2)]
        junk = sb("junk", [128, D], BF16)
        ss = sb("ss", [128, NT], F32)
        rstd = sb("rstd", [128, NT], F32)
        sq = sb("sq", [128, NT], F32)
        epst = sb("epst", [128, 1], F32)
        ps = [es.enter_context(nc.psum_tensor(pfx + f"ps{i}", [128, D], F32)) for i in range(2)]
        S = _sems(es, nc, [pfx + n for n in ("w", "gp", "ytf", "xf", "mm", "ss", "dv", "add", "out", "rs", "rc", "rc0")])
        s_w, s_gp, s_ytf, s_xf, s_mm, s_ss, s_dv, s_add, s_out, s_rs, s_rc, s_rc0 = [S[pfx + n] for n in ("w", "gp", "ytf", "xf", "mm", "ss", "dv", "add", "out", "rs", "rc", "rc0")]
        blk = es.enter_context(nc.Block())

        @blk.gpsimd
        def _(e):
            for q in range(4):
                e.dma_start(out=wo[:, :, q * 512:(q + 1) * 512],
                            in_=wout_d.ap().rearrange("(kc p) n -> p kc n", p=128)[:, :, q * 512:(q + 1) * 512]).then_inc(s_w, 16)
            for i in range(NT):
                e.wait_ge(s_dv, i + 1)
                e.wait_ge(s_xf, 16 * (i + 1))
                e.tensor_tensor(out=tt[i % 2][:], in0=tt[i % 2][:], in1=xt[i % 3][:], op=ALU.add).then_inc(s_add, 1)
                e.wait_ge(s_add, i + 1)
                e.dma_start(out=out_ap[i * 128:(i + 1) * 128, :], in_=tt[i % 2][:]).then_inc(s_out, 16)
            e.wait_ge(s_out, 16 * NT)

        @blk.sync
        def _(e):
            e.dma_start(out=gp[:], in_=gpost_d.ap().partition_broadcast(128)).then_inc(s_gp, 16)
            for g in range(NGp):
                if g >= 2:
                    e.wait_ge(s_mm, 4 * (g - 1))
                e.dma_start(out=yt[g % 2][:], in_=yT_d.ap()[:, :, g * 512:(g + 1) * 512].rearrange("kc p t -> p kc t")).then_inc(s_ytf, 16)
                for j in range(4):
                    i = g * 4 + j
                    if i >= 3:
                        e.wait_ge(s_add, i - 2)
                    e.dma_start(out=xt[i % 3][:], in_=x_ap[i * 128:(i + 1) * 128, :]).then_inc(s_xf, 16)

        @blk.tensor
        def _(e):
            e.wait_ge(s_w, 64)
            for i in range(NT):
                g, j = divmod(i, 4)
                if j == 0:
                    e.wait_ge(s_ytf, 16 * (g + 1))
                if i >= 2:
                    e.wait_ge(s_dv, i - 1)
                for kc in range(KC):
                    for cg in range(4):
                        mm = e.matmul(ps[i % 2][:, cg * 512:(cg + 1) * 512],
                                      lhsT=yt[g % 2][:, kc, j * 128:(j + 1) * 128],
                                      rhs=wo[:, kc, cg * 512:(cg + 1) * 512],
                                      start=(kc == 0), stop=(kc == KC - 1))
                mm.then_inc(s_mm, 1)

        @blk.scalar
        def _(e):
            e.wait_ge(s_rc0, 1)
            for i in range(NT):
                e.wait_ge(s_mm, i + 1)
                e.activation(out=junk[:], in_=ps[i % 2][:], func=AF.Square, scale=float(D ** -0.5),
                             accum_out=ss[:, i:i + 1]).then_inc(s_ss, 1)
                e.wait_ge(s_ss, i + 1)
                e.activation(out=sq[:, i:i + 1], in_=ss[:, i:i + 1], func=AF.Sqrt,
                             bias=epst[:, 0:1], scale=1.0).then_inc(s_rs, 1)

        @blk.vector
        def _(e):
            e.memset(epst[:], EPS).then_inc(s_rc0, 1)
            e.wait_ge(s_gp, 16)
            for i in range(NT):
                e.wait_ge(s_rs, i + 1)
                e.reciprocal(out=rstd[:, i:i + 1], in_=sq[:, i:i + 1]).then_inc(s_rc, 1)
                e.wait_ge(s_rc, i + 1)
                if i >= 2:
                    e.wait_ge(s_out, 16 * (i - 1))
                e.scalar_tensor_tensor(out=tt[i % 2][:], in0=ps[i % 2][:], scalar=rstd[:, i:i + 1], in1=gp[:],
                                       op0=ALU.mult, op1=ALU.mult).then_inc(s_dv, 1)


class NormCtx:
    def __init__(self, nc, P, sb, es, pfx, ntiles_total, ps, hnT, ident_d, gpre_d, x_rows):
        self.nc, self.P, self.ps, self.hnT, self.x_rows = nc, P, ps, hnT, x_rows
        self.xt = [sb(f"xt{i}", [128, D], F32) for i in range(3)]
        self.xs = [sb(f"xs{i}", [128, D], BF16) for i in range(2)]
        self.junk = sb("junk", [128, D], BF16)
        self.ident = sb("ident", [128, 128], BF16)
        self.gpre = sb("gpre", [128, KC], F32)
        self.epst = sb("epst", [128, 1], F32)
        self.ss = sb("ss", [128, ntiles_total], F32)
        self.sq = sb("sq", [128, ntiles_total], F32)
        self.rstd = sb("rstd", [128, ntiles_total], F32)
        self.c_xf = [P.dcs() for _ in range(3)]
        self.c_ss, self.c_rs, self.c_rc, self.c_xs, self.c_tp, self.c_ev = [P.cs() for _ in range(6)]
        self.c_init = P.dcs(2)
        self.c_init2 = P.cs()
        self.tk = {k: {} for k in ("xf", "ss", "rs", "rc", "xs", "tp", "ev")}
        self.t_init = P.op("sync", lambda e: [e.dma_start(out=self.ident[:], in_=ident_d.ap()),
                                              e.dma_start(out=self.gpre[:], in_=gpre_d.ap())], sig=self.c_init)
        self.t_init2 = P.op("vector", lambda e: e.memset(self.epst[:], EPS), sig=self.c_init2)

    def stage1(self, n, rows=128):
        P, tk = self.P, self.tk
        xt = self.xt[n % 3]
        r0 = self.x_rows(n)
        tk["xf"][n] = P.op("sync", lambda e: e.dma_start(out=xt[0:rows, :], in_=r0),
                           waits=[(self.c_xs, tk["xs"].get(n - 3))], sig=self.c_xf[n % 3])
        tk["ss"][n] = P.op("scalar", lambda e: e.activation(out=self.junk[0:rows, :], in_=xt[0:rows, :], func=AF.Square,
                                                             scale=float(D ** -0.5), accum_out=self.ss[0:rows, n:n + 1]),
                           waits=[(self.c_xf[n % 3], tk["xf"][n]), (self.c_init2, self.t_init2)], sig=self.c_ss)
        tk["rs"][n] = P.op("scalar", lambda e: e.activation(out=self.sq[0:rows, n:n + 1], in_=self.ss[0:rows, n:n + 1], func=AF.Sqrt,
                                                             bias=self.epst[0:rows, 0:1], scale=1.0),
                           waits=[(self.c_ss, tk["ss"][n])], sig=self.c_rs)
        tk["rc"][n] = P.op("vector", lambda e: e.reciprocal(out=self.rstd[0:rows, n:n + 1], in_=self.sq[0:rows, n:n + 1]),
                           waits=[(self.c_rs, tk["rs"][n])], sig=self.c_rc)

    def stage2(self, n, col0, extra_pe_waits=(), extra_ev_waits=(), rows=128, dst=None):
        import os
        dbg = int(os.environ.get("NORM_DBG", "9"))
        if dbg <= 1:
            return
        P, tk = self.P, self.tk
        xt, xs = self.xt[n % 3], self.xs[n % 2]
        pst = self.ps[n % 2][:].bitcast(BF16)
        tk["xs"][n] = P.op("scalar", lambda e: e.activation(out=xs[0:rows, :], in_=xt[0:rows, :], func=AF.Copy,
                                                             scale=self.rstd[0:rows, n:n + 1]),
                           waits=[(self.c_rc, tk["rc"][n]), (self.c_tp, tk["tp"].get(n - 2))], sig=self.c_xs)

        def tp(e):
            for kc in range(KC):
                ins = e.transpose(out=pst[:, kc * 128:kc * 128 + rows], in_=xs[0:rows, kc * 128:(kc + 1) * 128],
                                  identity=self.ident[0:rows, 0:rows])
            return ins
        if dbg <= 2:
            return
        tk["tp"][n] = P.op("tensor", tp, waits=[(self.c_xs, tk["xs"][n]), (self.c_ev, tk["ev"].get(n - 2)),
                                                (self.c_init, self.t_init)] + list(extra_pe_waits), sig=self.c_tp)
        if dbg <= 3:
            return
        gb = bass.AP(self.gpre, 0, [[KC, 128], [1, KC], [0, rows]])
        src = pst[:, 0:2048].rearrange("p (k t) -> p k t", k=KC)[:, :, 0:rows]
        hn = self.hnT if dst is None else dst
        tk["ev"][n] = P.op("vector", lambda e: e.tensor_tensor(out=hn[:, :, col0:col0 + rows], in0=src, in1=gb, op=ALU.mult),
                           waits=[(self.c_tp, tk["tp"][n])] + list(extra_ev_waits), sig=self.c_ev)


def attn_phase_a(nc, pfx, x_ap, ident_d, gpre_d, win_d, qT_d, kT_d, v_d, sg_d, NB, dbg_jobs=None, dbg_blocks=None):
    T_own = 2048 * NB
    NTT = 16 * (1 + NB)
    with contextlib.ExitStack() as es:
        P = Prog(nc, es, pfx)
        sb = lambda n, s, d: es.enter_context(nc.sbuf_tensor(pfx + n, s, d))
        hnT = sb("hnT", [128, KC, 2048], BF16)
        wt = [sb(f"wt{i}", [128, KC, 512], BF16) for i in range(3)]
        st = [sb(f"st{i}", [128, 2048], BF16) for i in range(4)]
        ps = [es.enter_context(nc.psum_tensor(pfx + f"ps{i}", [128, 2048], F32)) for i in range(2)]
        N = NormCtx(nc, P, sb, es, pfx, NTT, ps, hnT, ident_d, gpre_d, lambda n: x_ap[n * 128:(n + 1) * 128, :])
        c_wf = [P.dcs(4) for _ in range(3)]
        c_pj = P.cs()
        c_e = {"A": P.cs(), "V": P.cs()}
        c_so = [P.dcs() for _ in range(4)]
        w_src = win_d.ap().rearrange("(kc p) n -> p kc n", p=128)

        tk_wf, tk_pj, tk_e, tk_so, eng_of = {}, {}, {}, {}, {}
        last_job_of_w = {}
        jn = [0]
        wn = [0]
        last_evs = []

        def load_w(col0):
            if dbg_jobs is not None and jn[0] >= dbg_jobs:
                return -1
            w = wn[0]; wn[0] += 1
            slot = wt[w % 3]

            def f(e):
                return [e.dma_start(out=slot[:, 4 * q:4 * q + 4, :], in_=w_src[:, 4 * q:4 * q + 4, col0:col0 + 512]) for q in range(4)]
            tk_wf[w] = P.op("gpsimd", f, waits=[(c_pj, last_job_of_w.get(w - 3))], sig=c_wf[w % 3])
            return w

        def job(w, pe_fn, ev_eng, ev_fn, dma_fn, first_waits, ndma=1):
            if dbg_jobs is not None and jn[0] >= dbg_jobs:
                return
            j = jn[0]; jn[0] += 1
            psb = ps[j % 2]
            stb = st[j % 4]
            waits = [(c_wf[w % 3], tk_wf[w])] + list(first_waits)
            if j >= 2:
                waits.append((c_e[eng_of[j - 2]], tk_e[j - 2]))
            tk_pj[j] = P.op("tensor", lambda e: pe_fn(e, psb, wt[w % 3]), waits=waits, sig=c_pj)
            last_job_of_w[w] = tk_pj[j]
            eng_of[j] = ev_eng
            tk_e[j] = P.op("scalar" if ev_eng == "A" else "vector", lambda e: ev_fn(e, psb, stb),
                           waits=[(c_pj, tk_pj[j]), (c_so[j % 4], tk_so.get(j - 4))], sig=c_e[ev_eng])
            tk_so[j] = P.op("sync", lambda e: dma_fn(e, stb), waits=[(c_e[ev_eng], tk_e[j])], sig=c_so[j % 4], cnt=ndma)

        def fm_pe(c4, sbs):
            def f(e, psb, wtile):
                for kc in range(KC):
                    for s_ in sbs:
                        ins = e.matmul(psb[:, s_ * 512:(s_ + 1) * 512], lhsT=wtile[:, kc, c4 * 128:(c4 + 1) * 128],
                                       rhs=hnT[:, kc, s_ * 512:(s_ + 1) * 512], start=(kc == 0), stop=(kc == KC - 1))
                return ins
            return f

        def tm_pe(ttg):
            def f(e, psb, wtile):
                for b in range(4):
                    tt = ttg * 4 + b
                    for kc in range(KC):
                        ins = e.matmul(psb[:, b * 512:(b + 1) * 512], lhsT=hnT[:, kc, tt * 128:(tt + 1) * 128],
                                       rhs=wtile[:, kc, :], start=(kc == 0), stop=(kc == KC - 1))
                return ins
            return f

        def ev_copy(eng, d, part):
            def f(e, psb, stb):
                if part is None:
                    src, dst = psb[:, :], stb[:, :]
                else:
                    src, dst = psb[:, part * 512:(part + 1) * 512], stb[:, 0:512]
                if d > 1:
                    src = src.rearrange("p (u r) -> p u r", r=d)
                    dst = dst.rearrange("p (r u) -> p u r", r=d)
                if eng == "A":
                    return e.activation(out=dst, in_=src, func=AF.Copy)
                return e.tensor_copy(out=dst, in_=src)
            return f

        def ev_silu(e, psb, stb):
            return e.activation(out=stb[:, :], in_=psb[:, :], func=AF.Silu)

        toggle = [0]

        def next_eng():
            toggle[0] ^= 1
            return "A" if toggle[0] else "V"

        for tb in range(1 + NB if dbg_blocks is None else dbg_blocks):
            first_ev_waits = [(c_pj, tk_pj.get(jn[0] - 1))]
            pe_waits = [(c_e[eng_of[j]], tk_e[j]) for j in (jn[0] - 1, jn[0] - 2) if j >= 0]
            for t in range(16):
                n = tb * 16 + t
                N.stage1(n)
                if t >= 1:
                    N.stage2(n - 1, (t - 1) * 128, extra_pe_waits=pe_waits, extra_ev_waits=first_ev_waits)
            N.stage2(tb * 16 + 15, 15 * 128, extra_pe_waits=pe_waits, extra_ev_waits=first_ev_waits)
            fw = [(N.c_ev, N.tk["ev"].get(tb * 16 + 15))]
            for g, (_, d) in enumerate(GROUPS):
                full = (tb > 0) or (g == 2)
                sbs = [0, 1, 2, 3] if full else [3]
                part = None if full else 3
                ntok = 2048 if full else 512
                tok_lo = tb * 2048 + (0 if full else 1536)
                for qkv in ((0, 1, 2) if tb > 0 else (1, 2)):
                    for h4 in range(4):
                        col0 = ((qkv * 3 + g) * 16 + h4 * 4) * 128
                        w = load_w(col0)
                        if qkv < 2:
                            for c4 in range(4):
                                h = h4 * 4 + c4
                                if qkv == 0:
                                    dst = qT_d.ap()[g, h].rearrange("p (r u) -> p r u", r=d)
                                    u0 = (tok_lo - 2048) // d
                                else:
                                    dst = kT_d.ap()[g, h].rearrange("p (r u) -> p r u", r=d)
                                    u0 = tok_lo // d
                                dst = dst[:, :, u0:u0 + ntok // d]

                                def dma(e, stb, dst=dst, ntok=ntok, d=d):
                                    return e.dma_start(out=dst, in_=stb[:, 0:ntok].rearrange("p (r u) -> p r u", r=d))
                                eng = next_eng()
                                job(w, fm_pe(c4, sbs), eng, ev_copy(eng, d, part), dma, fw)
                        else:
                            for ttg in ([0, 1, 2, 3] if full else [3]):
                                t0 = tb * 2048 + ttg * 512
                                dsts = [v_d.ap()[g, h4 * 4:h4 * 4 + 4, t0 + b * 128:t0 + (b + 1) * 128, :].rearrange("hh p dh -> p hh dh")
                                        for b in range(4)]

                                def dma(e, stb, dsts=dsts):
                                    return [e.dma_start(out=dsts[b], in_=stb[:, b * 512:(b + 1) * 512].rearrange("p (hh dh) -> p hh dh", hh=4))
                                            for b in range(4)]
                                eng = next_eng()
                                job(w, tm_pe(ttg), eng, ev_copy(eng, 1, None), dma, fw, ndma=4)
            if tb > 0:
                for h4 in range(4):
                    w = load_w(QKVW + h4 * 512)
                    for c4 in range(4):
                        h = h4 * 4 + c4
                        dst = sg_d.ap()[h][:, (tb - 1) * 2048:tb * 2048]

                        def dma(e, stb, dst=dst):
                            return e.dma_start(out=dst, in_=stb[:, :])
                        job(w, fm_pe(c4, [0, 1, 2, 3]), "A", ev_silu, dma, fw)
        P.op("sync", None, waits=[(c_so[k % 4], tk_so[k]) for k in range(max(0, jn[0] - 4), jn[0])])
        P.run()


def attn_phase_b(nc, pfx, qT_d, kT_d, v_d, sg_d, bias_d, hv_d, yT_d, NB, heads=HEADS):
    T_own = 2048 * NB
    T_loc = 2048 * (1 + NB)
    NKB = T_loc // 128
    LA = 3
    with contextlib.ExitStack() as es:
        P = Prog(nc, es, pfx)
        sb = lambda n, s, d: es.enter_context(nc.sbuf_tensor(pfx + n, s, d))
        q_sb = [sb(f"q{i}", [128, T_own], BF16) for i in range(2)]
        k_sb = [sb(f"k{i}", [128, T_loc], BF16) for i in range(2)]
        v_sb = [sb(f"v{i}", [128, NKB, 128], BF16) for i in range(2)]
        b_sb = [sb(f"b{i}", [128, 256], F32) for i in range(2)]
        acc = [sb(f"acc{i}", [128, T_own], F32) for i in range(2)]
        rsb = [sb(f"rsb{i}", [128, T_own], F32) for i in range(2)]
        sg = [sb(f"sg{i}", [128, T_own], BF16) for i in range(2)]
        yt = [sb(f"yt{i}", [128, T_own], BF16) for i in range(2)]
        tmp = [sb(f"tmp{i}", [128, 256], F32) for i in range(4)]
        pt = [sb(f"pt{i}", [128, 256], BF16) for i in range(4)]
        ones = sb("ones", [128, 128], BF16)
        valid = sb("valid", [128, 128], BF16)
        s_ps = es.enter_context(nc.psum_tensor(pfx + "sps", [128, 4, 512], F32))
        oq_ps = es.enter_context(nc.psum_tensor(pfx + "oqps", [128, 4, 512], F32))

        c_ld = [P.dcs(4) for _ in range(2)]
        c_sg = [P.dcs() for _ in range(2)]
        c_yo = [P.dcs() for _ in range(2)]
        c_st, c_bi, c_ex, c_pv, c_qb, c_ac, c_rcp, c_p1, c_p2, c_i1 = [P.cs() for _ in range(10)]
        c_i2 = P.dcs()
        sc = sb("sc", [128, 1], F32)
        zt = sb("zt", [128, 256], F32)
        t_i1 = P.op("vector", lambda e: [e.memset(ones[:], 1.0), e.memset(sc[:], float(DH ** -0.5)), e.memset(zt[:], 0.0)], sig=c_i1, cnt=3)
        t_i2 = P.op("sync", lambda e: e.dma_start(out=valid[:], in_=hv_d.ap()), sig=c_i2)
        c_z = P.cs()
        tk_z = {}

        kbs = []
        units = []
        for h in range(heads):
            for g, (_, d) in enumerate(GROUPS):
                u = len(units)
                units.append((h, g, d))
                Uo, Ut = T_own // d, T_loc // d
                jq0 = (2048 // d) // 128
                jend = Ut // 128 - 1
                for r in range(d):
                    for j in range(jq0 - 1, jend + 1):
                        kbs.append(dict(u=u, h=h, g=g, d=d, r=r, j=j, cur=(j >= jq0), prv=(j + 1 <= jend),
                                        c=j - jq0, Uo=Uo, Ut=Ut, nblk=Ut // 128, halo=(j < jq0)))
        first_of_unit, last_of_unit = {}, {}
        for n, kb in enumerate(kbs):
            first_of_unit.setdefault(kb["u"], n)
            last_of_unit[kb["u"]] = n
        last_of_head = {kb["h"]: n for n, kb in enumerate(kbs)}

        tk_ld, tk_st, tk_bi, tk_ex, tk_ac, tk_sg, tk_yo, tk_p2 = {}, {}, {}, {}, {}, {}, {}, {}
        pt_free = {}
        unit_last_pe = {}
        qb_done = {}
        qb_info = {}
        pending_acc = []
        mctr = [0]
        open_qb = {}

        def plan_load(u):
            h, g, d = units[u]
            slot = u % 2
            prev = unit_last_pe.get(u - 2)
            vsrc = v_d.ap()[g, h].rearrange("(blk p r) dh -> p r blk dh", p=128, r=d)
            vdst = v_sb[slot][:, :, :].rearrange("p (r blk) dh -> p r blk dh", r=d)

            def f(e):
                return [e.dma_start(out=q_sb[slot][:, :], in_=qT_d.ap()[g, h]),
                        e.dma_start(out=k_sb[slot][:, :], in_=kT_d.ap()[g, h]),
                        e.dma_start(out=b_sb[slot][:, :], in_=bias_d.ap()[g * HEADS + h])] + \
                       [e.dma_start(out=vdst[:, r], in_=vsrc[:, r]) for r in range(d)]
            tk_ld[u] = P.op("sync", f, waits=[prev] if prev else [], sig=c_ld[slot], cnt=3 + d)
            if g == 0:
                hb = h % 2
                tk_sg[h] = P.op("sync", lambda e: e.dma_start(out=sg[hb][:, :], in_=sg_d.ap()[h]),
                                waits=[(c_p2, tk_p2.get(h - 2))], sig=c_sg[hb])

        def plan_st(n):
            kb = kbs[n]
            u, slot = kb["u"], n % 4
            us = u % 2
            r, j, c, Uo, Ut = kb["r"], kb["j"], kb["c"], kb["Uo"], kb["Ut"]
            if kb["cur"] and kb["prv"]:
                q0, qn, o0 = r * Uo + c * 128, 256, 0
            elif kb["prv"]:
                q0, qn, o0 = r * Uo, 128, 128
            else:
                q0, qn, o0 = r * Uo + c * 128, 128, 0
            kb["o0"], kb["qn"] = o0, qn
            waits = [(c_bi, tk_bi.get(n - 4))]
            if n == first_of_unit[u]:
                waits += [(c_ld[us], tk_ld[u])]
            tk_st[n] = P.op("tensor", lambda e: e.matmul(s_ps[:, slot, o0:o0 + qn], lhsT=k_sb[us][:, r * Ut + j * 128:r * Ut + (j + 1) * 128],
                                                        rhs=q_sb[us][:, q0:q0 + qn], start=True, stop=True),
                            waits=waits, sig=c_st)

        def plan_acc(m):
            h, g, d, r, c = qb_info[m]
            hb, ms = h % 2, m % 4
            lo = r + c * 128 * d
            a_ap = acc[hb][:, lo:lo + 127 * d + 1:d]
            r_ap = rsb[hb][:, lo:lo + 127 * d + 1:d]

            def f(e):
                if g == 0:
                    e.tensor_copy(out=a_ap, in_=oq_ps[:, ms, 0:128])
                    return e.tensor_copy(out=r_ap, in_=oq_ps[:, ms, 128:256])
                e.tensor_tensor(out=a_ap, in0=a_ap, in1=oq_ps[:, ms, 0:128], op=ALU.add)
                return e.tensor_tensor(out=r_ap, in0=r_ap, in1=oq_ps[:, ms, 128:256], op=ALU.add)
            waits = [(c_qb, qb_done[m])]
            if g == 0:
                waits.append((c_p2, tk_p2.get(h - 2)))
            tk_ac[m] = P.op("vector", f, waits=waits, sig=c_ac)

        import os
        bdbg = int(os.environ.get("B_DBG", "9"))

        def plan_rest(n):
            if bdbg <= 1:
                return
            kb = kbs[n]
            u, slot, us = kb["u"], n % 4, kb["u"] % 2
            h, g, d, r, j, c = kb["h"], kb["g"], kb["d"], kb["r"], kb["j"], kb["c"]
            o0, qn = kb["o0"], kb["qn"]
            tk_bi[n] = P.op("vector", lambda e: e.scalar_tensor_tensor(out=tmp[slot][:, o0:o0 + qn], in0=s_ps[:, slot, o0:o0 + qn],
                                                                       scalar=sc[:, 0:1], in1=b_sb[us][:, o0:o0 + qn],
                                                                       op0=ALU.mult, op1=ALU.add),
                            waits=[(c_st, tk_st[n]), (c_ex, tk_ex.get(n - 4)), (c_i1, t_i1)], sig=c_bi)
            if bdbg <= 2:
                return
            pf = pt_free.get(n - 4)
            tk_ex[n] = P.op("scalar", lambda e: e.activation(out=pt[slot][:, o0:o0 + qn], in_=tmp[slot][:, o0:o0 + qn], func=AF.Exp),
                            waits=[(c_bi, tk_bi[n])] + ([pf] if pf else []), sig=c_ex)
            if bdbg <= 3:
                return
            vblk = v_sb[us][:, r * kb["nblk"] + j, :]
            lone = valid if kb["halo"] else ones
            first = True
            if kb["cur"]:
                m = open_qb.pop((u, r))
                ms = m % 4

                def f(e, ms=ms):
                    e.matmul(oq_ps[:, ms, 0:128], lhsT=vblk, rhs=pt[slot][:, 0:128], start=False, stop=True, skip_group_check=True)
                    return e.matmul(oq_ps[:, ms, 128:256], lhsT=lone[:, :], rhs=pt[slot][:, 0:128], start=False, stop=True, skip_group_check=True)
                qb_done[m] = P.op("tensor", f, waits=[(c_ex, tk_ex[n]), (c_i1, t_i1), (c_i2, t_i2)], sig=c_qb)
                first = False
                pt_free[n] = (c_qb, qb_done[m])
                unit_last_pe[u] = (c_qb, qb_done[m])
                pending_acc.append((n + 2, m))
            if kb["prv"]:
                m = mctr[0]; mctr[0] += 1
                ms = m % 4
                open_qb[(u, r)] = m
                qb_info[m] = (h, g, d, r, c + 1)

                def f2(e, ms=ms):
                    e.matmul(oq_ps[:, ms, 0:128], lhsT=vblk, rhs=pt[slot][:, 128:256], start=True, stop=False, skip_group_check=True)
                    return e.matmul(oq_ps[:, ms, 128:256], lhsT=lone[:, :], rhs=pt[slot][:, 128:256], start=False, stop=False, skip_group_check=True)
                waits = [(c_ac, tk_ac.get(m - 4))]
                if first:
                    waits += [(c_ex, tk_ex[n]), (c_i1, t_i1), (c_i2, t_i2)]
                t = P.op("tensor", f2, waits=waits, sig=c_pv)
                pt_free[n] = (c_pv, t)
                unit_last_pe[u] = (c_pv, t)
            if bdbg <= 4:
                pending_acc.clear()
            while pending_acc and pending_acc[0][0] <= n:
                plan_acc(pending_acc.pop(0)[1])

        def plan_final(h):
            if bdbg <= 5:
                pending_acc.clear()
                return
            while pending_acc:
                plan_acc(pending_acc.pop(0)[1])
            hb = h % 2
            last_ac = c_ac.n
            t_r = P.op("vector", lambda e: e.reciprocal(out=rsb[hb][:, :], in_=rsb[hb][:, :]), waits=[(c_ac, last_ac)], sig=c_rcp)
            t1 = P.op("gpsimd", lambda e: e.tensor_tensor(out=acc[hb][:, :], in0=acc[hb][:, :], in1=rsb[hb][:, :], op=ALU.mult),
                      waits=[(c_rcp, t_r)], sig=c_p1)
            tk_p2[h] = P.op("gpsimd", lambda e: e.tensor_tensor(out=yt[hb][:, :], in0=acc[hb][:, :], in1=sg[hb][:, :], op=ALU.mult),
                            waits=[(c_p1, t1), (c_sg[hb], tk_sg[h]), (c_yo[hb], tk_yo.get(h - 2))], sig=c_p2)
            tk_yo[h] = P.op("gpsimd", lambda e: e.dma_start(out=yT_d.ap()[h], in_=yt[hb][:, :]), waits=[(c_p2, tk_p2[h])], sig=c_yo[hb])

        NK = len(kbs)
        for idx in range(NK + LA):
            if idx < NK:
                u = kbs[idx]["u"]
                if idx == first_of_unit[u]:
                    plan_load(u)
                plan_st(idx)
            n = idx - LA
            if n >= 0:
                plan_rest(n)
                if n == last_of_head[kbs[n]["h"]]:
                    plan_final(kbs[n]["h"])
        P.op("gpsimd", None, waits=[(c_yo[h % 2], tk_yo.get(h)) for h in range(max(0, heads - 2), heads)])
        P.run()


def attn_layer(nc, pfx, x_ap, x_own_ap, out_ap, consts, w, NB):
    T_own, T_loc = 2048 * NB, 2048 * (1 + NB)
    qT_d = nc.dram_tensor(pfx + "qT", [NG, HEADS, 128, T_own], BF16)
    kT_d = nc.dram_tensor(pfx + "kT", [NG, HEADS, 128, T_loc], BF16)
    v_d = nc.dram_tensor(pfx + "v", [NG, HEADS, T_loc, 128], BF16)
    sg_d = nc.dram_tensor(pfx + "sg", [HEADS, 128, T_own], BF16)
    yT_d = nc.dram_tensor(pfx + "yT", [HEADS, 128, T_own], BF16)
    attn_phase_a(nc, pfx + "a_", x_ap, consts["ident"], w["gpre"], w["w_in"], qT_d, kT_d, v_d, sg_d, NB)
    attn_phase_b(nc, pfx + "b_", qT_d, kT_d, v_d, sg_d, consts["bias"], consts["hv"], yT_d, NB)
    phase_c(nc, pfx + "c_", yT_d, x_own_ap, w["w_out"], w["gpost"], out_ap, T_own)


def conv_phase_a(nc, pfx, x_ap, xh_ap, ident_d, gpre_d, win_d, cw_d, yT_d, NB):
    NTT = 1 + 16 * NB
    with contextlib.ExitStack() as es:
        P = Prog(nc, es, pfx)
        sb = lambda n, s_, d: es.enter_context(nc.sbuf_tensor(pfx + n, s_, d))
        hnT = sb("hnT", [128, KC, 2048], BF16)
        hnH = sb("hnH", [128, KC, 2], BF16)
        wt = [sb(f"wt{i}", [128, KC, 4, 128], BF16) for i in range(2)]
        ublk = [sb(f"ub{i}", [128, 2 + 2048], F32) for i in range(2)]
        yst = [sb(f"ys{i}", [128, 2048], BF16) for i in range(2)]
        sgt = [sb(f"sg{i}", [128, 512], F32) for i in range(2)]
        hht = [sb(f"hh{i}", [128, 512], F32) for i in range(2)]
        bst = [sb(f"bs{i}", [128, 512], F32) for i in range(2)]
        cv = sb("cv", [128, 512], F32)
        carry = sb("carry", [128, KC, 2], F32)
        cw = sb("cw", [128, 48], F32)
        ps = [es.enter_context(nc.psum_tensor(pfx + f"ps{i}", [128, 2048], F32)) for i in range(2)]

        def x_rows(n):
            return xh_ap if n == 0 else x_ap[(n - 1) * 128:n * 128, :]
        N = NormCtx(nc, P, sb, es, pfx, NTT, ps, hnT, ident_d, gpre_d, x_rows)
        c_wf = [P.dcs(4) for _ in range(2)]
        c_pj, c_a, c_dv = P.cs(), P.cs(), P.cs()
        c_yo = [P.dcs() for _ in range(2)]
        c_cw = P.dcs()
        t_cw = P.op("sync", lambda e: e.dma_start(out=cw[:], in_=cw_d.ap()), sig=c_cw)
        w_src = win_d.ap().rearrange("(kc p) n -> p kc n", p=128)

        tk_wf, tk_pj, tk_a, tk_free, tk_yo = {}, {}, {}, {}, {}
        last_job_of_w = {}
        jn, wn = [0], [0]

        def load_w(c):
            w = wn[0]; wn[0] += 1
            slot = wt[w % 2]

            def f(e):
                return [e.dma_start(out=slot[:, :, part, :], in_=w_src[:, :, part * D + c * 128:part * D + (c + 1) * 128]) for part in range(4)]
            tk_wf[w] = P.op("gpsimd", f, waits=[(c_pj, last_job_of_w.get(w - 2))], sig=c_wf[w % 2])
            return w

        def pe_waits(j, w, first_waits):
            waits = [(c_wf[w % 2], tk_wf[w])] + list(first_waits)
            if j >= 2:
                waits.append((c_dv, tk_free[j - 2]))
            return waits

        N.stage1(0, rows=2)
        N.stage2(0, 0, rows=2, dst=hnH)
        fw = [(N.c_ev, N.tk["ev"][0])]
        for c in range(KC):
            w = load_w(c)
            j = jn[0]; jn[0] += 1
            psb = ps[j % 2]
            wtile = wt[w % 2]

            def pe(e, psb=psb, wtile=wtile):
                for part in (0, 2):
                    for kc in range(KC):
                        ins = e.matmul(psb[:, part * 512:part * 512 + 2], lhsT=wtile[:, kc, part, :], rhs=hnH[:, kc, 0:2],
                                       start=(kc == 0), stop=(kc == KC - 1))
                return ins
            tk_pj[j] = P.op("tensor", pe, waits=pe_waits(j, w, fw), sig=c_pj)
            last_job_of_w[w] = tk_pj[j]
            hb = hht[j % 2]
            prev_free = tk_free.get(j - 2)
            tk_a[j] = P.op("scalar", lambda e, psb=psb, hb=hb: e.activation(out=hb[:, 0:2], in_=psb[:, 0:2], func=AF.Copy),
                           waits=[(c_pj, tk_pj[j]), (c_dv, prev_free)], sig=c_a)
            tk_free[j] = P.op("vector", lambda e, psb=psb, hb=hb, c=c: e.tensor_tensor(out=carry[:, c, :], in0=psb[:, 1024:1026], in1=hb[:, 0:2], op=ALU.mult),
                              waits=[(c_a, tk_a[j])], sig=c_dv)

        for tb in range(NB):
            first_ev_waits = [(c_pj, tk_pj.get(jn[0] - 1))]
            pe_w = [(c_dv, tk_free[j]) for j in (jn[0] - 1, jn[0] - 2) if j >= 0]
            base = 1 + tb * 16
            for t in range(16):
                n = base + t
                N.stage1(n)
                if t >= 1:
                    N.stage2(n - 1, (t - 1) * 128, extra_pe_waits=pe_w, extra_ev_waits=first_ev_waits)
            N.stage2(base + 15, 15 * 128, extra_pe_waits=pe_w, extra_ev_waits=first_ev_waits)
            fw = [(N.c_ev, N.tk["ev"][base + 15])]
            for c in range(KC):
                w = load_w(c)
                ub = ublk[c % 2]
                ys = yst[c % 2]
                last_dv = None
                for sbk in range(4):
                    j = jn[0]; jn[0] += 1
                    psb = ps[j % 2]
                    wtile = wt[w % 2]
                    s0 = sbk * 512

                    def pe(e, psb=psb, wtile=wtile, s0=s0):
                        for part in range(4):
                            for kc in range(KC):
                                ins = e.matmul(psb[:, part * 512:(part + 1) * 512], lhsT=wtile[:, kc, part, :],
                                               rhs=hnT[:, kc, s0:s0 + 512], start=(kc == 0), stop=(kc == KC - 1))
                        return ins
                    tk_pj[j] = P.op("tensor", pe, waits=pe_waits(j, w, fw), sig=c_pj)
                    last_job_of_w[w] = tk_pj[j]
                    sg_, hh_, bs_ = sgt[j % 2], hht[j % 2], bst[j % 2]

                    def act(e, psb=psb, sg_=sg_, hh_=hh_):
                        return [e.activation(out=sg_[:, :], in_=psb[:, 1536:2048], func=AF.Silu),
                                e.activation(out=hh_[:, :], in_=psb[:, 0:512], func=AF.Copy)]
                    tk_a[j] = P.op("scalar", act, waits=[(c_pj, tk_pj[j]), (c_dv, tk_free.get(j - 2))], sig=c_a, cnt=2)
                    if sbk == 0:
                        t_cin = P.op("vector", lambda e, ub=ub, c=c: e.tensor_copy(out=ub[:, 0:2], in_=carry[:, c, :]),
                                     waits=[(c_dv, last_dv)], sig=c_dv)
                    t_u = P.op("vector", lambda e, psb=psb, hh_=hh_, ub=ub, s0=s0: e.tensor_tensor(out=ub[:, 2 + s0:2 + s0 + 512], in0=psb[:, 1024:1536], in1=hh_[:, :], op=ALU.mult),
                               waits=[(c_a, tk_a[j])], sig=c_dv)
                    t_bs = P.op("vector", lambda e, psb=psb, sg_=sg_, bs_=bs_: e.tensor_tensor(out=bs_[:, :], in0=psb[:, 512:1024], in1=sg_[:, :], op=ALU.mult),
                                sig=c_dv)
                    tk_free[j] = t_bs
                    t1 = P.op("vector", lambda e, ub=ub, s0=s0, c=c: e.tensor_scalar(out=cv[:, :], in0=ub[:, 2 + s0:2 + s0 + 512], scalar1=cw[:, c * 3 + 2:c * 3 + 3],
                                                                               scalar2=None, op0=ALU.mult),
                              waits=[(c_dv, t_u), (c_cw, t_cw)], sig=c_dv)
                    t2 = P.op("vector", lambda e, ub=ub, s0=s0, c=c: e.scalar_tensor_tensor(out=cv[:, :], in0=ub[:, 1 + s0:1 + s0 + 512], scalar=cw[:, c * 3 + 1:c * 3 + 2],
                                                                                      in1=cv[:, :], op0=ALU.mult, op1=ALU.add),
                              waits=[(c_dv, t1)], sig=c_dv)
                    t3 = P.op("vector", lambda e, ub=ub, s0=s0, c=c: e.scalar_tensor_tensor(out=cv[:, :], in0=ub[:, s0:s0 + 512], scalar=cw[:, c * 3:c * 3 + 1],
                                                                                      in1=cv[:, :], op0=ALU.mult, op1=ALU.add),
                              waits=[(c_dv, t2)], sig=c_dv)
                    ywaits = [(c_dv, t3)]
                    if sbk == 0:
                        ywaits.append((c_yo[c % 2], tk_yo.get((tb, c - 2)) if c >= 2 else tk_yo.get((tb - 1, c + KC - 2))))
                    t_y = P.op("vector", lambda e, ys=ys, bs_=bs_, s0=s0: e.tensor_tensor(out=ys[:, s0:s0 + 512], in0=bs_[:, :], in1=cv[:, :], op=ALU.mult),
                               waits=ywaits, sig=c_dv)
                    last_dv = t_y
                    if sbk == 3:
                        last_dv = P.op("vector", lambda e, ub=ub, c=c: e.tensor_copy(out=carry[:, c, :], in_=ub[:, 2048:2050]),
                                       waits=[(c_dv, t_u)], sig=c_dv)
                        dst = yT_d.ap()[c][:, tb * 2048:(tb + 1) * 2048]
                        tk_yo[(tb, c)] = P.op("sync", lambda e, ys=ys, dst=dst: e.dma_start(out=dst, in_=ys[:, :]),
                                              waits=[(c_dv, t_y)], sig=c_yo[c % 2])
        P.op("sync", None, waits=[(c_yo[c % 2], tk_yo[(NB - 1, c)]) for c in (KC - 2, KC - 1)])
        P.run()


def conv_layer(nc, pfx, x_ap, xh_ap, out_ap, consts, w, NB):
    T_own = 2048 * NB
    yT_d = nc.dram_tensor(pfx + "yT", [KC, 128, T_own], BF16)
    conv_phase_a(nc, pfx + "a_", x_ap, xh_ap, consts["ident"], w["gpre"], w["w_in"], w["cw"], yT_d, NB)
    phase_c(nc, pfx + "c_", yT_d, x_ap, w["w_out"], w["gpost"], out_ap, T_own)


NB_FULL = T_OWN // 2048
_PROG_CACHE = {}


def _build_attn_prog():
    nc = bass.Bass("TRN2", target_bir_lowering=False)
    x = nc.dram_tensor("x", [HALO + T_OWN, D], F32, kind="ExternalInput")
    consts = dict(ident=nc.dram_tensor("ident", [128, 128], BF16, kind="ExternalInput"),
                  bias=nc.dram_tensor("bias", [NG * HEADS, 128, 256], F32, kind="ExternalInput"),
                  hv=nc.dram_tensor("hv", [128, 128], BF16, kind="ExternalInput"))
    w = dict(gpre=nc.dram_tensor("gpre", [128, KC], F32, kind="ExternalInput"),
             w_in=nc.dram_tensor("w_in", [D, QKVW + D], F32, kind="ExternalInput"),
             w_out=nc.dram_tensor("w_out", [D, D], F32, kind="ExternalInput"),
             gpost=nc.dram_tensor("gpost", [1, D], F32, kind="ExternalInput"))
    out = nc.dram_tensor("out", [T_OWN, D], F32, kind="ExternalOutput")
    attn_layer(nc, "L_", x.ap(), x.ap()[HALO:, :], out.ap(), consts, w, NB_FULL)
    return nc


def _build_conv_prog():
    nc = bass.Bass("TRN2", target_bir_lowering=False)
    x = nc.dram_tensor("x", [T_OWN, D], F32, kind="ExternalInput")
    xh = nc.dram_tensor("xh", [2, D], F32, kind="ExternalInput")
    consts = dict(ident=nc.dram_tensor("ident", [128, 128], BF16, kind="ExternalInput"))
    w = dict(gpre=nc.dram_tensor("gpre", [128, KC], F32, kind="ExternalInput"),
             w_in=nc.dram_tensor("w_in", [D, 4 * D], F32, kind="ExternalInput"),
             cw=nc.dram_tensor("cw", [128, 48], F32, kind="ExternalInput"),
             w_out=nc.dram_tensor("w_out", [D, D], F32, kind="ExternalInput"),
             gpost=nc.dram_tensor("gpost", [1, D], F32, kind="ExternalInput"))
    out = nc.dram_tensor("out", [T_OWN, D], F32, kind="ExternalOutput")
    conv_layer(nc, "L_", x.ap(), xh.ap(), out.ap(), consts, w, NB_FULL)
    return nc


def _prog(kind):
    if kind not in _PROG_CACHE:
        _PROG_CACHE[kind] = _build_attn_prog() if kind == "attn" else _build_conv_prog()
    return _PROG_CACHE[kind]


def _f32(a):
    return np.ascontiguousarray(np.asarray(a, dtype=np.float32))


def kernel(**inputs):
    x = _f32(inputs["x"])
    cpb = SEQ // T_OWN
    ident = np.eye(128, dtype=np.float32).astype(ml_dtypes.bfloat16)
    btab = bias_table()
    ones_hv = np.ones((128, 128), ml_dtypes.bfloat16)
    zeros_hv = np.zeros((128, 128), ml_dtypes.bfloat16)
    cur = x
    for li in range(4):
        gpre = np.ascontiguousarray(_f32(inputs[f"l{li}_norm_pre"]).reshape(KC, 128).T)
        gpost = _f32(inputs[f"l{li}_norm_post"]).reshape(1, D)
        w_in = _f32(inputs[f"l{li}_w_in"])
        w_out = _f32(inputs[f"l{li}_w_out"])
        in_maps = []
        if li % 2 == 0:
            for c in range(NCORES):
                b, i = divmod(c, cpb)
                xin = np.zeros((HALO + T_OWN, D), np.float32)
                if i > 0:
                    xin[:HALO] = cur[b, i * T_OWN - HALO:i * T_OWN]
                xin[HALO:] = cur[b, i * T_OWN:(i + 1) * T_OWN]
                in_maps.append(dict(x=xin, ident=ident, bias=btab, hv=(ones_hv if i > 0 else zeros_hv),
                                    gpre=gpre, w_in=w_in, w_out=w_out, gpost=gpost))
            nc = _prog("attn")
        else:
            cw = np.ascontiguousarray(_f32(inputs[f"l{li}_conv_w"]).reshape(3, KC, 128).transpose(2, 1, 0).reshape(128, 48))
            for c in range(NCORES):
                b, i = divmod(c, cpb)
                xh = np.zeros((2, D), np.float32)
                if i > 0:
                    xh[:] = cur[b, i * T_OWN - 2:i * T_OWN]
                in_maps.append(dict(x=np.ascontiguousarray(cur[b, i * T_OWN:(i + 1) * T_OWN]), xh=xh, ident=ident,
                                    gpre=gpre, w_in=w_in, cw=cw, w_out=w_out, gpost=gpost))
            nc = _prog("conv")
        res = run_bass_kernel_spmd(nc, in_maps, core_ids=list(range(NCORES)))
        nxt = np.empty_like(x)
        for c in range(NCORES):
            b, i = divmod(c, cpb)
            nxt[b, i * T_OWN:(i + 1) * T_OWN] = res.results[c]["out"]
        cur = nxt
    return cur
```
